# Optimizing a Trainium2 kernel written in Bass

```python
import math
import jax, jax.numpy as jnp
from jax import lax
import numpy as np

D_MODEL = 1024
BATCH = 2
SEQ = 16384
DEPTH = 4

N_MIXERS = 2
N_NSA_LAYERS = (DEPTH + 1) // 2
N_S5_LAYERS = DEPTH // 2

N_HEADS = 16
HEAD_DIM = D_MODEL // N_HEADS
N_KV_GROUPS = 4
HEADS_PER_GROUP = N_HEADS // N_KV_GROUPS
KV_DIM = N_KV_GROUPS * HEAD_DIM
CMP_BLOCK = 32
CMP_STRIDE = 16
CMP_HIDDEN = 256
SEL_BLOCK = 64
SEL_TOPN = 16
WINDOW = 512
Q_BLOCK = 128
N_BRANCHES = 3
PROJ_COLS = D_MODEL + 6 * KV_DIM + N_HEADS * N_BRANCHES

S5_GROUP = 16
S5_GROUPS = D_MODEL // S5_GROUP
S5_STATE = 64

FFN_HIDDEN = ((8 * D_MODEL + 3 * 256 - 1) // (3 * 256)) * 256

EPS = 1e-6
NEG = -1e30
FORCE = 1e4

kernel_name = "hybrid_nsa_s5_interleaved_trunk"


def rmsnorm(x, g):
    xf = x.astype(jnp.float32)
    var = jnp.mean(xf * xf, axis=-1, keepdims=True)
    return xf * lax.rsqrt(var + EPS) * g.astype(jnp.float32)


def masked_softmax(s, mask):
    s = jnp.where(mask, s.astype(jnp.float32), NEG)
    return jax.nn.softmax(s, axis=-1) * mask


def compress_tokens(kv, pos, w1, b1, w2, b2):
    b, t = kv.shape[0], kv.shape[1]
    n_cmp = (t - CMP_BLOCK) // CMP_STRIDE + 1
    idx = jnp.arange(n_cmp)[:, None] * CMP_STRIDE + jnp.arange(CMP_BLOCK)[None, :]
    blocks = kv[:, idx] + pos[:, None, :]
    blocks = blocks.transpose(0, 1, 3, 2, 4).reshape(b, n_cmp, N_KV_GROUPS, CMP_BLOCK * HEAD_DIM)
    hid = jax.nn.gelu(blocks @ w1 + b1)
    return hid @ w2 + b2


def nsa_mixer(h, w_in, w_out, q_gain, k_gain, cmp_pos, cmp_w1, cmp_b1, cmp_w2, cmp_b2):
    b, t, _ = h.shape
    G, R, dh = N_KV_GROUPS, HEADS_PER_GROUP, HEAD_DIM
    scale = 1.0 / math.sqrt(dh)
    proj = h @ w_in
    cuts = [D_MODEL + i * KV_DIM for i in range(7)]
    q, kc_raw, vc_raw, ks_raw, vs_raw, kw_raw, vw_raw, g = jnp.split(proj, cuts, axis=-1)
    q = rmsnorm(q.reshape(b, t, N_HEADS, dh), q_gain)
    kvs = lambda a: a.reshape(b, t, G, dh)
    kc = rmsnorm(compress_tokens(kvs(kc_raw), cmp_pos[0], cmp_w1[0], cmp_b1[0], cmp_w2[0], cmp_b2[0]), k_gain[0])
    vc = compress_tokens(kvs(vc_raw), cmp_pos[1], cmp_w1[1], cmp_b1[1], cmp_w2[1], cmp_b2[1])
    ks = rmsnorm(kvs(ks_raw), k_gain[1])
    vs = kvs(vs_raw)
    kw = rmsnorm(kvs(kw_raw), k_gain[2])
    vw = kvs(vw_raw)
    gates = jax.nn.sigmoid(g.astype(jnp.float32))

    n_cmp = kc.shape[1]
    n_sblk = t // SEL_BLOCK
    top_n = min(SEL_TOPN, n_sblk)
    cmp_start = jnp.arange(n_cmp) * CMP_STRIDE
    cmp_end = cmp_start + CMP_BLOCK - 1
    sblk = jnp.arange(n_sblk)
    overlap = ((cmp_start[:, None] < (sblk[None, :] + 1) * SEL_BLOCK)
               & (cmp_start[:, None] + CMP_BLOCK > sblk[None, :] * SEL_BLOCK)).astype(jnp.float32)
    ks_blocks = ks.reshape(b, n_sblk, SEL_BLOCK, G, dh).transpose(0, 3, 1, 2, 4)
    vs_blocks = vs.reshape(b, n_sblk, SEL_BLOCK, G, dh).transpose(0, 3, 1, 2, 4)
    pad = ((0, 0), (WINDOW, 0), (0, 0), (0, 0))
    kw_pad = jnp.pad(kw, pad)
    vw_pad = jnp.pad(vw, pad)
    b_idx = jnp.arange(b)[:, None, None, None]
    g_idx = jnp.arange(G)[None, :, None, None]

    def block_fn(args):
        j, qb, gb = args
        tq = j * Q_BLOCK + jnp.arange(Q_BLOCK)
        s = jnp.einsum('bqgrd,bngd->bgrqn', qb, kc).astype(jnp.float32) * scale
        p_cmp = masked_softmax(s, cmp_end[None, :] <= tq[:, None])
        o_cmp = jnp.einsum('bgrqn,bngd->bqgrd', p_cmp, vc)
        p_sel = jnp.einsum('bgqn,ns->bgqs', p_cmp.sum(axis=2), overlap)
        cur = tq // SEL_BLOCK
        valid = sblk[None, :] * SEL_BLOCK <= tq[:, None]
        forced = (sblk[None, :] == 0) | (sblk[None, :] == cur[:, None]) | (sblk[None, :] == cur[:, None] - 1)
        score = jnp.where(forced, FORCE, jnp.where(valid, p_sel, NEG))
        vals, idx = lax.top_k(score, top_n)
        sel_ok = vals > 0.5 * NEG
        k_g = ks_blocks[b_idx, g_idx, idx].reshape(b, G, Q_BLOCK, top_n * SEL_BLOCK, dh)
        v_g = vs_blocks[b_idx, g_idx, idx].reshape(b, G, Q_BLOCK, top_n * SEL_BLOCK, dh)
        kpos = (idx[..., None] * SEL_BLOCK + jnp.arange(SEL_BLOCK)).reshape(b, G, Q_BLOCK, top_n * SEL_BLOCK)
        ok = jnp.repeat(sel_ok, SEL_BLOCK, axis=-1) & (kpos <= tq[:, None])
        s = jnp.einsum('bqgrd,bgqkd->bgrqk', qb, k_g).astype(jnp.float32) * scale
        p = masked_softmax(s, ok[:, :, None])
        o_sel = jnp.einsum('bgrqk,bgqkd->bqgrd', p, v_g)
        kb = lax.dynamic_slice_in_dim(kw_pad, j * Q_BLOCK, Q_BLOCK + WINDOW, axis=1)
        vb = lax.dynamic_slice_in_dim(vw_pad, j * Q_BLOCK, Q_BLOCK + WINDOW, axis=1)
        wpos = j * Q_BLOCK - WINDOW + jnp.arange(Q_BLOCK + WINDOW)
        m = (wpos[None, :] >= 0) & (wpos[None, :] <= tq[:, None]) & (tq[:, None] - wpos[None, :] < WINDOW)
        s = jnp.einsum('bqgrd,bkgd->bgrqk', qb, kb).astype(jnp.float32) * scale
        p = masked_softmax(s, m)
        o_win = jnp.einsum('bgrqk,bkgd->bqgrd', p, vb)
        o = gb[..., 0:1] * o_cmp + gb[..., 1:2] * o_sel + gb[..., 2:3] * o_win
        return o.reshape(b, Q_BLOCK, D_MODEL)

    nqb = t // Q_BLOCK
    q_blocks = q.reshape(b, nqb, Q_BLOCK, G, R, dh).transpose(1, 0, 2, 3, 4, 5)
    g_blocks = gates.reshape(b, nqb, Q_BLOCK, G, R, N_BRANCHES).transpose(1, 0, 2, 3, 4, 5)
    o = lax.map(block_fn, (jnp.arange(nqb), q_blocks, g_blocks))
    o = o.transpose(1, 0, 2, 3).reshape(b, t, D_MODEL)
    return o @ w_out


def complex_affine_combine(e1, e2):
    a1r, a1i, b1r, b1i = e1
    a2r, a2i, b2r, b2i = e2
    ar = a2r * a1r - a2i * a1i
    ai = a2r * a1i + a2i * a1r
    br = a2r * b1r - a2i * b1i + b2r
    bi = a2r * b1i + a2i * b1r + b2i
    return (ar, ai, br, bi)


def s5_mixer(h, w_in, b_re, b_im, c_re, c_im, d_skip, log_dt, a_re, a_im, w_glu):
    b, t, _ = h.shape
    f32 = jnp.float32
    u = (h @ w_in).astype(f32).reshape(b, t, S5_GROUPS, S5_GROUP)
    dt = jnp.exp(log_dt.astype(f32))[:, None]
    ar, ai = a_re.astype(f32), a_im.astype(f32)
    mag = jnp.exp(dt * ar)
    abar_r = mag * jnp.cos(dt * ai)
    abar_i = mag * jnp.sin(dt * ai)
    den = ar * ar + ai * ai
    coef_r = ((abar_r - 1.0) * ar + abar_i * ai) / den
    coef_i = (abar_i * ar - (abar_r - 1.0) * ai) / den
    br, bi = b_re.astype(f32), b_im.astype(f32)
    bbar_r = coef_r[..., None] * br - coef_i[..., None] * bi
    bbar_i = coef_r[..., None] * bi + coef_i[..., None] * br
    bu_r = jnp.einsum('btgc,gpc->btgp', u, bbar_r)
    bu_i = jnp.einsum('btgc,gpc->btgp', u, bbar_i)
    a_full_r = jnp.broadcast_to(abar_r, bu_r.shape)
    a_full_i = jnp.broadcast_to(abar_i, bu_r.shape)
    _, _, x_r, x_i = lax.associative_scan(complex_affine_combine, (a_full_r, a_full_i, bu_r, bu_i), axis=1)
    y = (jnp.einsum('btgp,gcp->btgc', x_r, c_re) - jnp.einsum('btgp,gcp->btgc', x_i, c_im)
         + d_skip.reshape(S5_GROUPS, S5_GROUP) * u)
    z = jax.nn.gelu(y.reshape(b, t, D_MODEL))
    vg = z @ w_glu
    return vg[..., :D_MODEL] * jax.nn.sigmoid(vg[..., D_MODEL:])


def swiglu(h, w_gate, w_up, w_down):
    return (jax.nn.silu(h @ w_gate) * (h @ w_up)) @ w_down


def setup_inputs(seed: int = 0) -> dict:
    key = jax.random.key(seed)
    ks = jax.random.split(key, 32)
    nrm = lambda k, s: jax.random.normal(k, s, jnp.float32)
    L, Ln, Ls = DEPTH, N_NSA_LAYERS, N_S5_LAYERS
    D, G, P, C = D_MODEL, S5_GROUPS, S5_STATE, S5_GROUP
    flat = CMP_BLOCK * HEAD_DIM
    return {
        "x": nrm(ks[0], (BATCH, SEQ, D)),
        "mix_norm": 1.0 + 0.01 * nrm(ks[1], (L, D)),
        "ffn_norm": 1.0 + 0.01 * nrm(ks[2], (L, D)),
        "nsa_w_in": nrm(ks[3], (Ln, D, PROJ_COLS)) * D ** -0.5,
        "nsa_w_out": nrm(ks[4], (Ln, D, D)) * D ** -0.5,
        "nsa_q_gain": 1.0 + 0.01 * nrm(ks[5], (Ln, HEAD_DIM)),
        "nsa_k_gain": 1.0 + 0.01 * nrm(ks[6], (Ln, N_BRANCHES, HEAD_DIM)),
        "nsa_cmp_pos": 0.1 * nrm(ks[7], (Ln, 2, CMP_BLOCK, HEAD_DIM)),
        "nsa_cmp_w1": nrm(ks[8], (Ln, 2, flat, CMP_HIDDEN)) * flat ** -0.5,
        "nsa_cmp_b1": 0.01 * nrm(ks[9], (Ln, 2, CMP_HIDDEN)),
        "nsa_cmp_w2": nrm(ks[10], (Ln, 2, CMP_HIDDEN, HEAD_DIM)) * CMP_HIDDEN ** -0.5,
        "nsa_cmp_b2": 0.01 * nrm(ks[11], (Ln, 2, HEAD_DIM)),
        "s5_w_in": nrm(ks[12], (Ls, D, D)) * D ** -0.5,
        "s5_b_re": nrm(ks[13], (Ls, G, P, C)) * (2 * C) ** -0.5,
        "s5_b_im": nrm(ks[14], (Ls, G, P, C)) * (2 * C) ** -0.5,
        "s5_c_re": nrm(ks[15], (Ls, G, C, P)) * (2 * P) ** -0.5,
        "s5_c_im": nrm(ks[16], (Ls, G, C, P)) * (2 * P) ** -0.5,
        "s5_d": nrm(ks[17], (Ls, D)),
        "s5_log_dt": jax.random.uniform(ks[18], (Ls, G), jnp.float32, math.log(0.001), math.log(0.1)),
        "s5_a_re": -0.5 + 0.01 * nrm(ks[19], (Ls, G, P)),
        "s5_a_im": math.pi * jnp.arange(P, dtype=jnp.float32) + 0.01 * nrm(ks[20], (Ls, G, P)),
        "s5_w_glu": nrm(ks[21], (Ls, D, 2 * D)) * D ** -0.5,
        "ffn_w_gate": nrm(ks[22], (L, D, FFN_HIDDEN)) * D ** -0.5,
        "ffn_w_up": nrm(ks[23], (L, D, FFN_HIDDEN)) * D ** -0.5,
        "ffn_w_down": nrm(ks[24], (L, FFN_HIDDEN, D)) * FFN_HIDDEN ** -0.5,
    }


def reference(x, mix_norm, ffn_norm, nsa_w_in, nsa_w_out, nsa_q_gain, nsa_k_gain, nsa_cmp_pos,
              nsa_cmp_w1, nsa_cmp_b1, nsa_cmp_w2, nsa_cmp_b2, s5_w_in, s5_b_re, s5_b_im, s5_c_re,
              s5_c_im, s5_d, s5_log_dt, s5_a_re, s5_a_im, s5_w_glu, ffn_w_gate, ffn_w_up, ffn_w_down):
    for layer in range(DEPTH):
        h = rmsnorm(x, mix_norm[layer])
        i = layer // N_MIXERS
        if layer % N_MIXERS == 0:
            mix = nsa_mixer(h, nsa_w_in[i], nsa_w_out[i], nsa_q_gain[i], nsa_k_gain[i], nsa_cmp_pos[i],
                            nsa_cmp_w1[i], nsa_cmp_b1[i], nsa_cmp_w2[i], nsa_cmp_b2[i])
        else:
            mix = s5_mixer(h, s5_w_in[i], s5_b_re[i], s5_b_im[i], s5_c_re[i], s5_c_im[i], s5_d[i],
                           s5_log_dt[i], s5_a_re[i], s5_a_im[i], s5_w_glu[i])
        x = x + mix.astype(x.dtype)
        h = rmsnorm(x, ffn_norm[layer])
        x = x + swiglu(h, ffn_w_gate[layer], ffn_w_up[layer], ffn_w_down[layer]).astype(x.dtype)
    return x
```

```python
import math
import numpy as np
import concourse.bass as bass
import concourse.mybir as mybir
from concourse.bass_utils import run_bass_kernel_spmd
from contextlib import ExitStack

F32 = mybir.dt.float32
BF16 = mybir.dt.bfloat16
AF = mybir.ActivationFunctionType
ALU = mybir.AluOpType
AX = mybir.AxisListType


_UID = [0]


class Prog:
    ENG = ['pe', 'act', 'dve', 'pool', 'sp']
    BLK = {'pe': 'tensor', 'act': 'scalar', 'dve': 'vector', 'pool': 'gpsimd', 'sp': 'sync'}

    def __init__(self, nc, stack):
        self.nc = nc
        self.st = stack
        _UID[0] += 1
        self.uid = _UID[0]
        self.q = {e: [] for e in self.ENG}
        self.cnt = {e: 0 for e in self.ENG}
        self.sem = {e: stack.enter_context(nc.semaphore('s%d_%s' % (self.uid, e))) for e in self.ENG}
        self.lastw = {}
        self.readers = {}
        self.dsem = {}
        self.nt = 0

    def sb(self, shape, dt, name=None):
        self.nt += 1
        return self.st.enter_context(self.nc.sbuf_tensor(name or ('t%d_%d' % (self.uid, self.nt)), list(shape), dt))

    def ps(self, shape, dt=F32, name=None):
        self.nt += 1
        return self.st.enter_context(self.nc.psum_tensor(name or ('p%d_%d' % (self.uid, self.nt)), list(shape), dt))

    def _deps(self, reads, writes):
        deps = {}

        def add(tok):
            if deps.get(tok[0], 0) < tok[1]:
                deps[tok[0]] = tok[1]
        for k in reads:
            if k in self.lastw:
                add(self.lastw[k])
        for k in writes:
            if k in self.lastw:
                add(self.lastw[k])
            for sk, v in self.readers.get(k, {}).items():
                add((sk, v))
        return deps

    def _commit(self, tok, reads, writes):
        for k in reads:
            r = self.readers.setdefault(k, {})
            if r.get(tok[0], 0) < tok[1]:
                r[tok[0]] = tok[1]
        for k in writes:
            self.lastw[k] = tok
            self.readers[k] = {}

    def op(self, e, meth, reads, writes, *args, **kw):
        fn = (meth, args, kw)
        deps = self._deps(reads, writes)
        self.cnt[e] += 1
        self.q[e].append((deps, fn, None))
        self._commit((e, self.cnt[e]), reads, writes)

    def dma(self, e, key, reads, writes, **kw):
        fn = ('dma_start', (), kw)
        deps = self._deps(reads, writes)
        if key not in self.dsem:
            self.dsem[key] = [self.st.enter_context(self.nc.semaphore('d%d_%d' % (self.uid, len(self.dsem)))), 0]
        s = self.dsem[key]
        s[1] += 16
        self.q[e].append((deps, fn, s[0]))
        self._commit((('d', key), s[1]), reads, writes)

    def emit(self):
        with self.nc.Block() as block:
            for e in self.ENG:
                def body(eng, e=e):
                    waited = {}
                    for deps, fn, dm in self.q[e]:
                        for sk, v in deps.items():
                            if sk == e and e == 'pe':
                                continue
                            if waited.get(sk, 0) < v:
                                sem = self.sem[sk] if isinstance(sk, str) else self.dsem[sk[1]][0]
                                eng.wait_ge(sem, v)
                                waited[sk] = v
                        ins = getattr(eng, fn[0])(*fn[1], **fn[2])
                        if dm is None:
                            ins.then_inc(self.sem[e], 1)
                        else:
                            ins.then_inc(dm, 16)
                    if e == 'sp':
                        for k, (sem, v) in self.dsem.items():
                            eng.wait_ge(sem, v)
                getattr(block, self.BLK[e])(body)


D = 1024
EPS = 1e-6
NEGB = -1e30
FORCE = 1e4
BIGB = 30000.0
NQC = 652


def make_ident(P, ident, ones_f):
    P.op('pool', 'memset', [], ['ones_f'], ones_f[:], 1.0)
    P.op('pool', 'affine_select', ['ones_f'], ['ident'], out=ident[:], in_=ones_f[:], pattern=[[-1, 128]],
         compare_op=ALU.is_equal, fill=0.0, base=0, channel_multiplier=1)


def nsa_phaseA(nc, T, xT, gm, w_in, qg, kg, featT_d, vtok_d, gates_d):
    with ExitStack() as st:
        P = Prog(nc, st)
        w_sb = P.sb([128, 8, NQC], BF16); g_sb = P.sb([128, 8], F32)
        ones = P.sb([128, 128], BF16); ones_f = P.sb([128, 128], F32); ident = P.sb([128, 128], BF16)
        GT = P.sb([128, 640], F32)
        xt = [P.sb([128, 8, 512], F32) for _ in range(2)]
        xsq = P.sb([128, 8, 512], BF16); h = P.sb([128, 8, 512], BF16)
        rs = P.sb([128, 512], F32); rstd = P.sb([128, 512], F32)
        proj = [P.sb([128, NQC], F32) for _ in range(2)]
        sq = P.sb([128, 640], F32); ss10 = P.sb([128, 10], F32); rs10 = P.sb([128, 10], F32); ri10 = P.sb([128, 10], F32)
        tmp = P.sb([128, 640], F32); pn = [P.sb([128, 640], BF16) for _ in range(2)]
        TT = [P.sb([64, 8, 512], BF16) for _ in range(2)]
        vt = [P.sb([128, 4, 2, 64], BF16) for _ in range(2)]
        gsb = [P.sb([128, 4, 12], F32) for _ in range(2)]
        psb = [P.ps([128, 512]) for _ in range(8)]
        P.op('pool', 'memset', [], ['ones'], ones[:], 1.0)
        make_ident(P, ident, ones_f)
        P.op('pool', 'memset', [], ['GT'], GT[:], 1.0)
        for hh in range(4):
            P.dma('sp', 'GT', [], ['GT'], out=GT[:, hh*64:(hh+1)*64], in_=qg.partition_broadcast(128))
        P.dma('sp', 'GT', [], ['GT'], out=GT[:, 384:448], in_=kg[1, :].partition_broadcast(128))
        P.dma('sp', 'GT', [], ['GT'], out=GT[:, 512:576], in_=kg[2, :].partition_broadcast(128))
        P.dma('sp', 'g', [], ['g'], out=g_sb[:], in_=gm[:, :])
        for c in range(8):
            P.dma('pool', 'w', [], ['w'], out=w_sb[:, c, :], in_=w_in[c*128:(c+1)*128, :])
        xTv = xT.rearrange("(c p) t -> p c t", p=128)
        nb = T // 512

        def load(i):
            s = i % 2
            P.dma('sp', ('xt', s), [], [('xt', s)], out=xt[s][:], in_=xTv[:, :, i*512:(i+1)*512])
        load(0)
        k = 0
        for i in range(nb):
            s = i % 2
            if i + 1 < nb:
                load(i + 1)
            X = xt[s]
            P.op('act', 'activation', [('xt', s)], ['xsq'], out=xsq[:], in_=X[:], func=AF.Square)
            for c in range(8):
                P.op('pe', 'matmul', ['xsq', 'ones'], [('ps', 0)], psb[0][:], lhsT=ones[:], rhs=xsq[:, c, :], start=(c == 0), stop=(c == 7))
            P.op('act', 'activation', [('ps', 0)], ['rs'], out=rs[:], in_=psb[0][:], func=AF.Sqrt, scale=1.0/D, bias=EPS)
            P.op('dve', 'reciprocal', ['rs'], ['rstd'], out=rstd[:], in_=rs[:])
            for c in range(8):
                P.op('dve', 'scalar_tensor_tensor', [('xt', s), 'g', 'rstd'], [('h', c)], out=h[:, c, :], in0=X[:, c, :], scalar=g_sb[:, c:c+1], in1=rstd[:], op0=ALU.mult, op1=ALU.mult)
            for sub in range(4):
                kk = k % 2; k += 1
                bA, bB, bT = 1 + kk, 3 + kk, 5 + kk
                pr = proj[kk]
                for c in range(8):
                    P.op('pe', 'matmul', ['w', ('h', c)], [('ps', bA)], psb[bA][:], lhsT=h[:, c, sub*128:(sub+1)*128], rhs=w_sb[:, c, 0:512], start=(c == 0), stop=(c == 7))
                for c in range(8):
                    P.op('pe', 'matmul', ['w', ('h', c)], [('ps', bB)], psb[bB][:, 0:140], lhsT=h[:, c, sub*128:(sub+1)*128], rhs=w_sb[:, c, 512:652], start=(c == 0), stop=(c == 7))
                P.op('act', 'copy', [('ps', bA)], [('pr', kk)], out=pr[:, 0:512], in_=psb[bA][:])
                P.op('dve', 'tensor_copy', [('ps', bB)], [('pr', kk)], out=pr[:, 512:652], in_=psb[bB][:, 0:140])
                P.op('pool', 'tensor_tensor', [('pr', kk)], ['sq'], out=sq[:], in0=pr[:, 0:640], in1=pr[:, 0:640], op=ALU.mult)
                P.op('dve', 'tensor_reduce', ['sq'], ['ss10'], out=ss10[:], in_=sq[:].rearrange("p (g d) -> p g d", d=64), axis=AX.X, op=ALU.add)
                P.op('act', 'activation', ['ss10'], ['rs10'], out=rs10[:], in_=ss10[:], func=AF.Sqrt, scale=1.0/64, bias=EPS)
                P.op('dve', 'reciprocal', ['rs10'], ['ri10'], out=ri10[:], in_=rs10[:])
                P.op('dve', 'memset', [], ['ri10'], ri10[:, 4:6], 1.0)
                P.op('dve', 'memset', [], ['ri10'], ri10[:, 7:8], 1.0)
                P.op('dve', 'memset', [], ['ri10'], ri10[:, 9:10], 1.0)
                P.op('dve', 'tensor_tensor', [('pr', kk), 'ri10'], ['tmp'], out=tmp[:].rearrange("p (g d) -> p g d", d=64), in0=pr[:, 0:640].rearrange("p (g d) -> p g d", d=64),
                     in1=ri10[:].unsqueeze(2).to_broadcast([128, 10, 64]), op=ALU.mult)
                P.op('pool', 'tensor_tensor', ['tmp', 'GT'], [('pn', kk)], out=pn[kk][:], in0=tmp[:], in1=GT[:], op=ALU.mult)
                P.op('act', 'activation', [('pr', kk)], [('gsb', s)], out=gsb[s][:, sub, :], in_=pr[:, 640:652], func=AF.Sigmoid)
                psT = psb[bT][:].bitcast(BF16)
                cols = [0, 64, 128, 192, 256, 320, 384, 512]
                for e, c0 in enumerate(cols):
                    P.op('pe', 'transpose', [('pn', kk), 'ident'], [('ps', bT)], out=psT[0:64, e*128:(e+1)*128], in_=pn[kk][:, c0:c0+64], identity=ident[:])
                P.op('act', 'copy', [('ps', bT)], [('TT', s)], out=TT[s][:, :, sub*128:(sub+1)*128], in_=psT[0:64, :].rearrange("p (e t) -> p e t", t=128))
                P.op('pool', 'tensor_copy', [('pn', kk)], [('vt', s)], out=vt[s][:, sub, 0, :], in_=pn[kk][:, 448:512])
                P.op('pool', 'tensor_copy', [('pn', kk)], [('vt', s)], out=vt[s][:, sub, 1, :], in_=pn[kk][:, 576:640])
            t0 = i * 512
            P.dma('sp', ('TT', s), [('TT', s)], [], out=featT_d.rearrange("e d t -> d e t")[:, :, t0:t0+512], in_=TT[s][:])
            P.dma('sp', ('vt', s), [('vt', s)], [], out=vtok_d[t0:t0+512].rearrange("(s p) e d -> p s e d", p=128), in_=vt[s][:])
            P.dma('sp', ('gsb', s), [('gsb', s)], [], out=gates_d[t0:t0+512].rearrange("(s p) g -> p s g", p=128), in_=gsb[s][:])
        P.emit()


def nsa_phaseB(nc, T, featT_d, posT, w1, b1T, w2, b2, kg, kcc_d, vcc_d):
    NCP = T // 16
    NW = min(512, NCP)
    with ExitStack() as st:
        P = Prog(nc, st)
        raw = P.sb([64, T + 32], BF16)
        W1 = P.sb([64, 32, 256], BF16); pT = P.sb([64, 32], BF16)
        b1s = P.sb([128, 2], F32); bias1 = P.sb([128, 2], F32)
        w2s = P.sb([128, 2, 64], BF16); b2b = P.sb([128, 64], F32); kg0 = P.sb([128, 64], F32)
        hid = P.sb([128, 2, NCP], BF16)
        kv = P.sb([128, 64], F32); sq = P.sb([128, 64], F32); ss = P.sb([128, 1], F32); rs = P.sb([128, 1], F32); ri = P.sb([128, 1], F32)
        kn = P.sb([128, 64], BF16)
        kccT = P.sb([64, NCP], BF16); vcc = P.sb([128, NCP // 128, 64], BF16)
        ones_f = P.sb([128, 128], F32); ident = P.sb([128, 128], BF16)
        psb = [P.ps([128, 512]) for _ in range(4)]
        make_ident(P, ident, ones_f)
        P.dma('sp', 'kg0', [], ['kg0'], out=kg0[:], in_=kg[0, :].partition_broadcast(128))
        raw3 = raw[:].rearrange("p (n s) -> p n s", s=16)
        bk = [0]
        for i in range(2):
            P.dma('sp', 'raw', [], ['raw'], out=raw[:, 0:T], in_=featT_d[4 + i, :, :])
            P.op('pool', 'memset', [], ['raw'], raw[:, T:T+32], 0.0)
            w1v = w1[i].rearrange("(j d) h -> d j h", d=64)
            for jj in range(4):
                P.dma('pool', 'W1', [], ['W1'], out=W1[:, jj*8:(jj+1)*8, :], in_=w1v[:, jj*8:(jj+1)*8, :])
            P.dma('pool', 'pT', [], ['pT'], out=pT[:], in_=posT[i, :, :])
            P.dma('sp', 'b1s', [], ['b1s'], out=b1s[:], in_=b1T[i, :, :])
            P.dma('pool', 'w2s', [], ['w2s'], out=w2s[:], in_=w2[i].rearrange("(c p) d -> p c d", p=128))
            P.dma('sp', 'b2b', [], ['b2b'], out=b2b[:], in_=b2[i, :].partition_broadcast(128))
            for hc in range(2):
                b = bk[0] % 4; bk[0] += 1
                for j in range(32):
                    P.op('pe', 'matmul', ['W1', 'pT'], [('ps', b)], psb[b][:, 0:1], lhsT=W1[:, j, hc*128:(hc+1)*128], rhs=pT[:, j:j+1], start=(j == 0), stop=(j == 31))
                P.op('dve', 'tensor_tensor', [('ps', b), 'b1s'], ['bias1'], out=bias1[:, hc:hc+1], in0=b1s[:, hc:hc+1], in1=psb[b][:, 0:1], op=ALU.add)
            for nch in range(NCP // NW):
                n0 = nch * NW
                for hc in range(2):
                    b = bk[0] % 4; bk[0] += 1
                    for j in range(32):
                        P.op('pe', 'matmul', ['W1', 'raw'], [('ps', b)], psb[b][:, 0:NW], lhsT=W1[:, j, hc*128:(hc+1)*128],
                             rhs=raw3[:, n0 + j // 16: n0 + j // 16 + NW, j % 16], start=(j == 0), stop=(j == 31))
                    P.op('act', 'activation', [('ps', b), 'bias1'], ['hid'], out=hid[:, hc, n0:n0+NW], in_=psb[b][:, 0:NW], func=AF.Gelu_apprx_tanh, bias=bias1[:, hc:hc+1])
            for c in range(NCP // 128):
                b = bk[0] % 4; bk[0] += 1
                for hc in range(2):
                    P.op('pe', 'matmul', ['hid', 'w2s'], [('ps', b)], psb[b][:, 0:64], lhsT=hid[:, hc, c*128:(c+1)*128], rhs=w2s[:, hc, :], start=(hc == 0), stop=(hc == 1))
                P.op('dve', 'tensor_tensor', [('ps', b), 'b2b'], ['kv'], out=kv[:], in0=b2b[:], in1=psb[b][:, 0:64], op=ALU.add)
                if i == 0:
                    P.op('dve', 'tensor_tensor', ['kv'], ['sq'], out=sq[:], in0=kv[:], in1=kv[:], op=ALU.mult)
                    P.op('dve', 'tensor_reduce', ['sq'], ['ss'], out=ss[:], in_=sq[:], axis=AX.X, op=ALU.add)
                    P.op('act', 'activation', ['ss'], ['rs'], out=rs[:], in_=ss[:], func=AF.Sqrt, scale=1.0/64, bias=EPS)
                    P.op('dve', 'reciprocal', ['rs'], ['ri'], out=ri[:], in_=rs[:])
                    P.op('dve', 'scalar_tensor_tensor', ['kv', 'ri', 'kg0'], ['kn'], out=kn[:], in0=kv[:], scalar=ri[:, 0:1], in1=kg0[:], op0=ALU.mult, op1=ALU.mult)
                    b2_ = bk[0] % 4; bk[0] += 1
                    psT = psb[b2_][:].bitcast(BF16)
                    P.op('pe', 'transpose', ['kn', 'ident'], [('ps', b2_)], out=psT[0:64, 0:128], in_=kn[:], identity=ident[:])
                    P.op('act', 'copy', [('ps', b2_)], ['kccT'], out=kccT[:, c*128:(c+1)*128], in_=psT[0:64, 0:128])
                else:
                    P.op('act', 'copy', ['kv'], ['vcc'], out=vcc[:, c, :], in_=kv[:])
        P.dma('sp', 'o1', ['kccT'], [], out=kcc_d[:, :], in_=kccT[:])
        P.dma('sp', 'o2', ['vcc'], [], out=vcc_d.rearrange("(c p) d -> p c d", p=128), in_=vcc[:])
        P.emit()


def nsa_phaseC(nc, T, featT_d, vtok_d, gates_d, kcc_d, vcc_d, oT):
    NS = T // 64; NU = (NS + 63) // 64; NQ = T // 128; NCP = T // 16; NCC = NCP // 128
    NSP = NU * 64
    SC = 0.125
    with ExitStack() as st:
        P = Prog(nc, st)
        KsA = P.sb([128, T], BF16); KwT = P.sb([64, T], BF16); KcT = P.sb([64, NCP], BF16)
        Vs = P.sb([128, NQ, 65], BF16); Vw = P.sb([128, NQ, 65], BF16)
        VcA = P.sb([128, NCC, 65], BF16); OV = P.sb([128, NCC, NS], BF16)
        gt = P.sb([128, NQ, 12], F32)
        ones_f = P.sb([128, 128], F32); ident = P.sb([128, 128], BF16)
        Bpad = P.sb([128, 64 + NSP + 64], BF16)
        QB = [[P.sb([128, 512], BF16) for _ in range(NU)] for _ in range(2)]
        PTb = [P.sb([128, 512], BF16) for _ in range(3)]
        sc = P.sb([128, NS], F32); sc2 = P.sb([128, NS], F32); m1 = P.sb([128, 8], F32); m2 = P.sb([128, 8], F32); thr = P.sb([128, 1], F32)
        dn = P.sb([128, 4], F32); rd = P.sb([128, 4], F32); cf = P.sb([128, 4], F32)
        oacc = P.sb([128, 4, 64], F32); tmp = P.sb([128, 4, 64], F32); ob = P.sb([128, 256], BF16)
        oTs = [P.sb([128, 2, 128], F32) for _ in range(2)]
        psb = [P.ps([128, 512]) for _ in range(8)]
        make_ident(P, ident, ones_f)
        P.dma('sp', 'KsA', [], ['KsA'], out=KsA[0:64, :], in_=featT_d[6, :, :])
        P.op('pool', 'memset', [], ['KsB'], KsA[64:128, :], BIGB)
        if T >= 4096:
            pat = [[0, T // 4096], [1, 64], [0, 64]]
        else:
            pat = [[1, NS], [0, 64]]
        P.op('pool', 'affine_select', ['KsB'], ['KsB'], out=KsA[64:128, :], in_=KsA[64:128, :], pattern=pat, compare_op=ALU.is_equal, fill=0.0, base=0, channel_multiplier=-1)
        P.dma('sp', 'KwT', [], ['KwT'], out=KwT[:], in_=featT_d[7, :, :])
        P.dma('sp', 'KcT', [], ['KcT'], out=KcT[:], in_=kcc_d[:, :])
        P.op('pool', 'memset', [], ['Vs'], Vs[:, :, 64:65], 1.0)
        P.op('pool', 'memset', [], ['Vw'], Vw[:, :, 64:65], 1.0)
        P.op('pool', 'memset', [], ['VcA'], VcA[:, :, 64:65], 1.0)
        vv = vtok_d.rearrange("(c p) e d -> p c e d", p=128)
        for c0 in range(0, NQ, 16):
            c1 = min(NQ, c0 + 16)
            P.dma('sp', 'Vs', [], ['Vs'], out=Vs[:, c0:c1, 0:64], in_=vv[:, c0:c1, 0, :])
            P.dma('sp', 'Vw', [], ['Vw'], out=Vw[:, c0:c1, 0:64], in_=vv[:, c0:c1, 1, :])
        P.dma('sp', 'VcA', [], ['VcA'], out=VcA[:, :, 0:64], in_=vcc_d.rearrange("(c p) d -> p c d", p=128))
        P.dma('sp', 'gt', [], ['gt'], out=gt[:], in_=gates_d.rearrange("(j p) g -> p j g", p=128))
        P.op('pool', 'memset', [], ['OV'], OV[:], 1.0)
        P.op('pool', 'affine_select', ['OV'], ['OV'], out=OV[:], in_=OV[:], pattern=[[128, NCC], [-4, NS]], compare_op=ALU.is_ge, fill=0.0, base=1, channel_multiplier=1)
        P.op('pool', 'affine_select', ['OV'], ['OV'], out=OV[:], in_=OV[:], pattern=[[-128, NCC], [4, NS]], compare_op=ALU.is_ge, fill=0.0, base=3, channel_multiplier=-1)
        P.op('pool', 'memset', [], ['Bpad'], Bpad[:], -1.0)
        qv = featT_d[0:4, :, :].rearrange("h d t -> d h t")
        cnt = {'s': 0, 'p': 0}

        def sbank():
            b = cnt['s'] % 3; cnt['s'] += 1
            return b

        def ptbuf():
            k = cnt['p'] % 3; cnt['p'] += 1
            return k

        def pv(bank, width, PT, k, rhs, rkey, first, last, off_fn):
            for hh in range(4):
                o0 = off_fn(hh)
                P.op('pe', 'matmul', [('pt', k), rkey], [('ps', bank(hh))], psb[bank(hh)][:, o0:o0+width], lhsT=PTb[k][:, hh*128:(hh+1)*128], rhs=rhs,
                     start=(first and o0 == 0), stop=last, skip_group_check=True)

        def finish(bank, br, j, first):
            pv3 = psb[bank][:, 0:260].rearrange("p (h e) -> p h e", e=65)
            P.op('dve', 'tensor_scalar_max', [('ps', bank)], ['dn'], out=dn[:], in0=pv3[:, :, 64], scalar1=1e-30)
            P.op('dve', 'reciprocal', ['dn'], ['rd'], out=rd[:], in_=dn[:])
            P.op('dve', 'tensor_tensor', ['rd', 'gt'], ['cf'], out=cf[:], in0=rd[:], in1=gt[:, j, br::3], op=ALU.mult)
            dst = oacc if first else tmp
            P.op('dve', 'tensor_tensor', [('ps', bank), 'cf'], ['oacc' if first else 'tmp'], out=dst[:], in0=pv3[:, :, 0:64], in1=cf[:].unsqueeze(2).to_broadcast([128, 4, 64]), op=ALU.mult)

        for j in range(NQ):
            qs = j % 2
            nu = (2 * j + 1) // 64 + 1
            q0 = 128 * j
            for u in range(nu):
                P.dma('sp', ('QBq', qs, u), [], [('QBq', qs, u)], out=QB[qs][u][0:64, :].rearrange("p (h q) -> p h q", q=128), in_=qv[:, :, q0:q0+128])
            ncc = (8 * j + 6) // 128 + 1
            for cc in range(ncc):
                b = sbank(); k = ptbuf()
                P.op('pe', 'matmul', ['KcT', ('QBq', qs, 0)], [('ps', b)], psb[b][:], lhsT=KcT[:, cc*128:(cc+1)*128], rhs=QB[qs][0][0:64, :], start=True, stop=True)
                P.op('act', 'activation', [('ps', b)], [('pt', k)], out=PTb[k][:], in_=psb[b][:], func=AF.Exp, scale=SC)
                if j <= 16 * cc + 16:
                    P.op('pool', 'affine_select', [('pt', k)], [('pt', k)], out=PTb[k][:].rearrange("p (h q) -> p h q", q=128), in_=PTb[k][:].rearrange("p (h q) -> p h q", q=128),
                         pattern=[[0, 4], [1, 128]], compare_op=ALU.is_ge, fill=0.0, base=128*j - 2048*cc - 31, channel_multiplier=-16)
                pv(lambda hh: 3, 65, None, k, VcA[:, cc, :], 'VcA', cc == 0, cc == ncc - 1, lambda hh: hh * 65)
                pv(lambda hh: 4 + hh // 2, NS, None, k, OV[:, cc, :], 'OV', cc == 0, cc == ncc - 1, lambda hh: (hh % 2) * 256)
            finish(3, 0, j, True)
            P.op('dve', 'tensor_scalar', [('ps', 4), 'rd'], ['sc'], out=sc[:], in0=psb[4][:, 0:NS], scalar1=rd[:, 0:1], scalar2=None, op0=ALU.mult)
            for hh in range(1, 4):
                P.op('dve', 'scalar_tensor_tensor', [('ps', 4 + hh // 2), 'rd', 'sc'], ['sc'], out=sc[:], in0=psb[4 + hh // 2][:, (hh % 2)*256:(hh % 2)*256+NS],
                     scalar=rd[:, hh:hh+1], in1=sc[:], op0=ALU.mult, op1=ALU.add)
            if 2 * j + 2 < NS:
                P.op('dve', 'memset', [], ['sc'], sc[:, 2*j+2:NS], NEGB)
            P.op('dve', 'memset', [], ['sc'], sc[0:64, 2*j+1:2*j+2], NEGB)
            P.op('dve', 'memset', [], ['sc'], sc[64:128, 2*j+1:2*j+2], FORCE)
            P.op('dve', 'memset', [], ['sc'], sc[:, 2*j:2*j+1], FORCE)
            if j >= 1:
                P.op('dve', 'memset', [], ['sc'], sc[0:64, 2*j-1:2*j], FORCE)
            P.op('dve', 'memset', [], ['sc'], sc[:, 0:1], FORCE)
            P.op('dve', 'max', ['sc'], ['m1'], out=m1[:], in_=sc[:])
            P.op('dve', 'match_replace', ['sc', 'm1'], ['sc2'], out=sc2[:], in_to_replace=m1[:], in_values=sc[:], imm_value=NEGB)
            P.op('dve', 'max', ['sc2'], ['m2'], out=m2[:], in_=sc2[:])
            P.op('dve', 'tensor_scalar_max', ['m2'], ['thr'], out=thr[:], in0=m2[:, 7:8], scalar1=-1e29)
            P.op('dve', 'tensor_scalar', ['sc', 'thr'], ['Bpad'], out=Bpad[:, 64:64+NS], in0=sc[:], scalar1=thr[:, 0:1], scalar2=1.0, op0=ALU.is_ge, op1=ALU.subtract)
            psT = psb[4][:].bitcast(BF16)
            for u in range(nu):
                P.op('pe', 'transpose', ['Bpad', 'ident'], [('ps', 4)], out=psT[:, u*128:(u+1)*128], in_=Bpad[:, 64*u:64*u+128], identity=ident[:])
            for u in range(nu):
                P.op('dve', 'tensor_copy', [('ps', 4)], [('QBb', qs, u)], out=QB[qs][u][64:128, :].rearrange("p (h q) -> p h q", q=128),
                     in_=psT[64:128, u*128:(u+1)*128].unsqueeze(1).to_broadcast([64, 4, 128]))
            for c in range(j + 1):
                b = sbank(); k = ptbuf(); u = c // 32
                P.op('pe', 'matmul', ['KsA', 'KsB', ('QBq', qs, u), ('QBb', qs, u)], [('ps', b)], psb[b][:], lhsT=KsA[:, c*128:(c+1)*128], rhs=QB[qs][u][:, :], start=True, stop=True)
                P.op('act', 'activation', [('ps', b)], [('pt', k)], out=PTb[k][:], in_=psb[b][:], func=AF.Exp, scale=SC)
                if c == j:
                    P.op('pool', 'affine_select', [('pt', k)], [('pt', k)], out=PTb[k][:].rearrange("p (h q) -> p h q", q=128), in_=PTb[k][:].rearrange("p (h q) -> p h q", q=128),
                         pattern=[[0, 4], [1, 128]], compare_op=ALU.is_ge, fill=0.0, base=0, channel_multiplier=-1)
                pv(lambda hh: 6, 65, None, k, Vs[:, c, :], 'Vs', c == 0, c == j, lambda hh: hh * 65)
            cw0 = max(0, j - 4)
            for c in range(cw0, j + 1):
                b = sbank(); k = ptbuf()
                P.op('pe', 'matmul', ['KwT', ('QBq', qs, 0)], [('ps', b)], psb[b][:], lhsT=KwT[:, c*128:(c+1)*128], rhs=QB[qs][0][0:64, :], start=True, stop=True)
                P.op('act', 'activation', [('ps', b)], [('pt', k)], out=PTb[k][:], in_=psb[b][:], func=AF.Exp, scale=SC)
                if c == j:
                    P.op('pool', 'affine_select', [('pt', k)], [('pt', k)], out=PTb[k][:].rearrange("p (h q) -> p h q", q=128), in_=PTb[k][:].rearrange("p (h q) -> p h q", q=128),
                         pattern=[[0, 4], [1, 128]], compare_op=ALU.is_ge, fill=0.0, base=0, channel_multiplier=-1)
                if c == j - 4:
                    P.op('pool', 'affine_select', [('pt', k)], [('pt', k)], out=PTb[k][:].rearrange("p (h q) -> p h q", q=128), in_=PTb[k][:].rearrange("p (h q) -> p h q", q=128),
                         pattern=[[0, 4], [-1, 128]], compare_op=ALU.is_ge, fill=0.0, base=-1, channel_multiplier=1)
                pv(lambda hh: 7, 65, None, k, Vw[:, c, :], 'Vw', c == cw0, c == j, lambda hh: hh * 65)
            finish(6, 1, j, False)
            P.op('pool', 'tensor_tensor', ['oacc', 'tmp'], ['oacc'], out=oacc[:], in0=oacc[:], in1=tmp[:], op=ALU.add)
            finish(7, 2, j, False)
            P.op('pool', 'tensor_tensor', ['oacc', 'tmp'], ['ob'], out=ob[:], in0=oacc[:].rearrange("p h d -> p (h d)"), in1=tmp[:].rearrange("p h d -> p (h d)"), op=ALU.add)
            psO = psb[5][:].bitcast(BF16)
            for c2 in range(2):
                P.op('pe', 'transpose', ['ob', 'ident'], [('ps', 5)], out=psO[:, c2*128:(c2+1)*128], in_=ob[:, c2*128:(c2+1)*128], identity=ident[:])
            P.op('act', 'copy', [('ps', 5)], [('oTs', qs)], out=oTs[qs][:], in_=psO[:, 0:256].rearrange("p (c q) -> p c q", q=128))
            P.dma('sp', ('oTs', qs), [('oTs', qs)], [], out=oT.rearrange("(c p) t -> p c t", p=128)[:, :, q0:q0+128], in_=oTs[qs][:])
        P.emit()


D = 1024
EPS = 1e-6
TWO_PI = 2 * math.pi
MAGIC = 12582912.0


def s5_phaseA(nc, T, xT, gm, w_in, ud):
    M = T // 8
    BW = min(T, 2048)
    with ExitStack() as st:
        P = Prog(nc, st)
        w_sb = P.sb([128, 8, 256], BF16); g_sb = P.sb([128, 8], F32); ones = P.sb([128, 128], BF16)
        xt = [P.sb([128, 8, 512], F32) for _ in range(2)]
        xsq = P.sb([128, 8, 512], BF16); h = P.sb([128, 8, 512], BF16)
        rs = P.sb([128, 512], F32); rstd = P.sb([128, 512], F32)
        stg = [P.sb([128, 2, 8, BW // 8], F32) for _ in range(2)]
        psb = [P.ps([128, 512]) for _ in range(5)]
        P.op('pool', 'memset', [], ['ones'], ones[:], 1.0)
        P.dma('sp', 'g', [], ['g'], out=g_sb[:], in_=gm[:, :])
        for c in range(8):
            P.dma('pool', 'w', [], ['w'], out=w_sb[:, c, :], in_=w_in[c*128:(c+1)*128, :])
        xTv = xT.rearrange("(c p) t -> p c t", p=128)
        nb = T // 512

        def load(i):
            s = i % 2
            P.dma('sp', ('xt', s), [], [('xt', s)], out=xt[s][:], in_=xTv[:, :, i*512:(i+1)*512])
        load(0)
        kb = 0
        for i in range(nb):
            s = i % 2
            if i + 1 < nb:
                load(i + 1)
            X = xt[s]
            P.op('act', 'activation', [('xt', s)], ['xsq'], out=xsq[:], in_=X[:], func=AF.Square)
            for c in range(8):
                P.op('pe', 'matmul', ['xsq', 'ones'], [('ps', 0)], psb[0][:], lhsT=ones[:], rhs=xsq[:, c, :], start=(c == 0), stop=(c == 7))
            P.op('act', 'activation', [('ps', 0)], ['rs'], out=rs[:], in_=psb[0][:], func=AF.Sqrt, scale=1.0/D, bias=EPS)
            P.op('dve', 'reciprocal', ['rs'], ['rstd'], out=rstd[:], in_=rs[:])
            for c in range(8):
                P.op('dve', 'scalar_tensor_tensor', [('xt', s), 'g', 'rstd'], [('h', c)], out=h[:, c, :], in0=X[:, c, :], scalar=g_sb[:, c:c+1], in1=rstd[:], op0=ALU.mult, op1=ALU.mult)
            blk = (i * 512) // BW; sb_ = blk % 2; off = ((i * 512) % BW) // 8
            for ct in range(2):
                b = 1 + kb % 4; kb += 1
                for c in range(8):
                    P.op('pe', 'matmul', ['w', ('h', c)], [('ps', b)], psb[b][:], lhsT=w_sb[:, c, ct*128:(ct+1)*128], rhs=h[:, c, :], start=(c == 0), stop=(c == 7))
                P.op('act' if ct == 0 else 'dve', 'copy' if ct == 0 else 'tensor_copy', [('ps', b)], [('stg', sb_)], out=stg[sb_][:, ct, :, off:off+64], in_=psb[b][:].rearrange("p (m s) -> p s m", s=8))
            if (i * 512 + 512) % BW == 0:
                m0 = blk * (BW // 8)
                for ct in range(2):
                    for gl in range(8):
                        g = ct * 8 + gl
                        P.dma('sp', ('stg', sb_), [('stg', sb_)], [], out=ud[g, :, :, m0:m0 + BW // 8].rearrange("s c m -> c s m"), in_=stg[sb_][16*gl:16*gl+16, ct, :, :])
        P.emit()


def s5_phaseB(nc, T, ud, arP, aiP, ldtP, bPr, bPi, cPr, cPi, dtile, yd, dbg=None):
    M = T // 8
    MC = min(512, M)
    NST = int(math.ceil(math.log2(M)))
    with ExitStack() as st:
        P = Prog(nc, st)
        cnt = [0]

        def T_(shape, dt=F32):
            return P.sb(shape, dt)
        ar = T_([128, 8]); ai = T_([128, 8]); ldt = T_([128, 8])
        bR = T_([128, 8, 16]); bI = T_([128, 8, 16]); cR = T_([128, 8, 16]); cI = T_([128, 8, 16])
        for nm, tl, src in [('ar', ar, arP), ('ai', ai, aiP), ('ldt', ldt, ldtP)]:
            P.dma('sp', nm, [], [nm], out=tl[:], in_=src[:, :])
        for nm, tl, src in [('bR', bR, bPr), ('bI', bI, bPi), ('cR', cR, cPr), ('cI', cI, cPi)]:
            P.dma('sp', nm, [], [nm], out=tl[:], in_=src[:, :, :])
        ones_f = T_([128, 128]); identb = P.sb([128, 128], BF16); ident = T_([128, 128])
        make_ident(P, identb, ones_f)
        P.op('pool', 'affine_select', ['ones_f'], ['identf'], out=ident[:], in_=ones_f[:], pattern=[[-1, 128]], compare_op=ALU.is_equal, fill=0.0, base=0, channel_multiplier=1)
        cmask = T_([128, 8, 16])
        P.op('pool', 'affine_select', ['ones_f'], ['cmask'], out=cmask[:], in_=ones_f[:].rearrange("p (i c) -> p i c", c=16), pattern=[[16, 8], [0, 16]],
             compare_op=ALU.is_ge, fill=0.0, base=15, channel_multiplier=-1)

        nk = [0]

        def tt(out, a, b, op, eng='dve'):
            nk[0] += 1
            P.op(eng, 'tensor_tensor', [('v', id(a.tensor)), ('v', id(b.tensor))], [('v', id(out.tensor))], out=out, in0=a, in1=b, op=op)

        def ts(out, a, s1, op0, s2=None, op1=None):
            kw = dict(out=out, in0=a, scalar1=s1, scalar2=s2, op0=op0)
            if op1 is not None:
                kw['op1'] = op1
            P.op('dve', 'tensor_scalar', [('v', id(a.tensor))], [('v', id(out.tensor))], **kw)

        def act(out, a, func, **kw):
            P.op('act', 'activation', [('v', id(a.tensor))], [('v', id(out.tensor))], out=out, in_=a, func=func, **kw)
        for nm, tl in [('ar', ar), ('ai', ai), ('ldt', ldt), ('bR', bR), ('bI', bI), ('cR', cR), ('cI', cI)]:
            P.op('pool', 'tensor_copy', [nm], [('v', id(tl))], out=tl[:], in_=tl[:])

        def cmul(or_, oi_, ar_, ai_, br_, bi_, t1, t2):
            tt(t1, ar_, br_, ALU.mult); tt(t2, ai_, bi_, ALU.mult); tt(or_, t1, t2, ALU.subtract)
            tt(t1, ar_, bi_, ALU.mult); tt(t2, ai_, br_, ALU.mult); tt(oi_, t1, t2, ALU.add)

        dt = T_([128, 8]); lr = T_([128, 8]); li = T_([128, 8]); mag = T_([128, 8])
        act(dt[:], ldt[:], AF.Exp)
        tt(lr[:], dt[:], ar[:], ALU.mult); tt(li[:], dt[:], ai[:], ALU.mult)
        act(mag[:], lr[:], AF.Exp)
        sn = T_([128, 8]); cs = T_([128, 8]); k1 = T_([128, 8]); r1 = T_([128, 8])

        xs_ = T_([128, 8])

        def sin_of(out, xin, shift):
            ts(xs_[:], xin, shift, ALU.add)
            ts(k1[:], xs_[:], 1.0 / TWO_PI, ALU.mult, MAGIC, ALU.add)
            ts(k1[:], k1[:], MAGIC, ALU.subtract)
            ts(r1[:], k1[:], -6.28125, ALU.mult)
            tt(r1[:], r1[:], xs_[:], ALU.add)
            ts(k1[:], k1[:], -(TWO_PI - 6.28125), ALU.mult)
            tt(r1[:], r1[:], k1[:], ALU.add)
            ts(r1[:], r1[:], math.pi, ALU.min, -math.pi, ALU.max)
            act(out, r1[:], AF.Sin)
        sin_of(sn[:], li[:], 0.0)
        sin_of(cs[:], li[:], math.pi / 2)
        abr = T_([128, 8]); abi = T_([128, 8])
        tt(abr[:], mag[:], cs[:], ALU.mult); tt(abi[:], mag[:], sn[:], ALU.mult)
        t1 = T_([128, 8]); t2 = T_([128, 8]); den = T_([128, 8]); e1 = T_([128, 8]); cr = T_([128, 8]); ci = T_([128, 8])
        tt(t1[:], ar[:], ar[:], ALU.mult); tt(t2[:], ai[:], ai[:], ALU.mult); tt(den[:], t1[:], t2[:], ALU.add)
        P.op('dve', 'reciprocal', [('v', id(den))], [('v', id(den))], out=den[:], in_=den[:])
        ts(e1[:], abr[:], -1.0, ALU.add)
        tt(t1[:], e1[:], ar[:], ALU.mult); tt(t2[:], abi[:], ai[:], ALU.mult); tt(cr[:], t1[:], t2[:], ALU.add); tt(cr[:], cr[:], den[:], ALU.mult)
        tt(t1[:], abi[:], ar[:], ALU.mult); tt(t2[:], e1[:], ai[:], ALU.mult); tt(ci[:], t1[:], t2[:], ALU.subtract); tt(ci[:], ci[:], den[:], ALU.mult)
        if dbg is not None:
            for ii, tl in enumerate([abr, abi, cr, ci]):
                P.dma('sp', 'dbg', [('v', id(tl))], [], out=dbg[ii, :, :], in_=tl[:])
        bbr = T_([128, 8, 16]); bbi = T_([128, 8, 16]); u1 = T_([128, 8, 16]); u2 = T_([128, 8, 16])
        crb = cr[:].unsqueeze(2).to_broadcast([128, 8, 16]); cib = ci[:].unsqueeze(2).to_broadcast([128, 8, 16])
        cmul(bbr[:], bbi[:], crb, cib, bR[:], bI[:], u1[:], u2[:])
        PWr = T_([128, 8, 9]); PWi = T_([128, 8, 9]); IPr = T_([128, 8, 8]); IPi = T_([128, 8, 8]); PVr = T_([128, 8, 8]); PVi = T_([128, 8, 8])
        P.op('dve', 'memset', [], [('v', id(PWr))], PWr[:, :, 0:1], 1.0)
        P.op('dve', 'memset', [], [('v', id(PWi))], PWi[:, :, 0:1], 0.0)
        P.op('dve', 'tensor_copy', [('v', id(abr))], [('v', id(PWr))], out=PWr[:, :, 1], in_=abr[:])
        P.op('dve', 'tensor_copy', [('v', id(abi))], [('v', id(PWi))], out=PWi[:, :, 1], in_=abi[:])
        for k in range(2, 9):
            cmul(PWr[:, :, k], PWi[:, :, k], PWr[:, :, k-1], PWi[:, :, k-1], abr[:], abi[:], t1[:], t2[:])
        for s in range(8):
            P.op('dve', 'tensor_copy', [('v', id(PWr))], [('v', id(PVr))], out=PVr[:, :, s], in_=PWr[:, :, 7 - s])
            P.op('dve', 'tensor_copy', [('v', id(PWi))], [('v', id(PVi))], out=PVi[:, :, s], in_=PWi[:, :, 7 - s])
        m2 = T_([128, 8]); ivr = T_([128, 8]); ivi = T_([128, 8])
        tt(t1[:], abr[:], abr[:], ALU.mult); tt(t2[:], abi[:], abi[:], ALU.mult); tt(m2[:], t1[:], t2[:], ALU.add)
        P.op('dve', 'reciprocal', [('v', id(m2))], [('v', id(m2))], out=m2[:], in_=m2[:])
        tt(ivr[:], abr[:], m2[:], ALU.mult); tt(ivi[:], abi[:], m2[:], ALU.mult); ts(ivi[:], ivi[:], -1.0, ALU.mult)
        P.op('dve', 'tensor_copy', [('v', id(ivr))], [('v', id(IPr))], out=IPr[:, :, 0], in_=ivr[:])
        P.op('dve', 'tensor_copy', [('v', id(ivi))], [('v', id(IPi))], out=IPi[:, :, 0], in_=ivi[:])
        for s in range(1, 8):
            cmul(IPr[:, :, s], IPi[:, :, s], IPr[:, :, s-1], IPi[:, :, s-1], ivr[:], ivi[:], t1[:], t2[:])
        Xr = T_([128, 8, 8, 16]); Xi = T_([128, 8, 8, 16]); Zr = T_([128, 8, 8, 16]); Zi = T_([128, 8, 8, 16])
        Yr = T_([128, 8, 8, 16]); Yi = T_([128, 8, 8, 16]); NYi = T_([128, 8, 8, 16]); w1_ = T_([128, 8, 8, 16]); w2_ = T_([128, 8, 8, 16])
        sh = [128, 8, 8, 16]
        bb_r = bbr[:].unsqueeze(2).to_broadcast(sh); bb_i = bbi[:].unsqueeze(2).to_broadcast(sh)
        cmul(Xr[:], Xi[:], IPr[:].unsqueeze(3).to_broadcast(sh), IPi[:].unsqueeze(3).to_broadcast(sh), bb_r, bb_i, w1_[:], w2_[:])
        cmul(Zr[:], Zi[:], PVr[:].unsqueeze(3).to_broadcast(sh), PVi[:].unsqueeze(3).to_broadcast(sh), bb_r, bb_i, w1_[:], w2_[:])
        cmul(Yr[:], Yi[:], PWr[:, :, 1:9].unsqueeze(3).to_broadcast(sh), PWi[:, :, 1:9].unsqueeze(3).to_broadcast(sh),
             cR[:].unsqueeze(2).to_broadcast(sh), cI[:].unsqueeze(2).to_broadcast(sh), w1_[:], w2_[:])
        ts(NYi[:], Yi[:], -1.0, ALU.mult)
        Ybr = P.sb(sh, F32); Ybn = P.sb(sh, F32)
        P.op('dve', 'tensor_copy', [('v', id(Yr))], ['Ybr'], out=Ybr[:], in_=Yr[:])
        P.op('dve', 'tensor_copy', [('v', id(NYi))], ['Ybn'], out=Ybn[:], in_=NYi[:])
        ASr = T_([128, 8, NST]); ASi = T_([128, 8, NST]); NASi = T_([128, 8, NST])
        P.op('dve', 'tensor_copy', [('v', id(PWr))], [('v', id(ASr))], out=ASr[:, :, 0], in_=PWr[:, :, 8])
        P.op('dve', 'tensor_copy', [('v', id(PWi))], [('v', id(ASi))], out=ASi[:, :, 0], in_=PWi[:, :, 8])
        for j in range(1, NST):
            cmul(ASr[:, :, j], ASi[:, :, j], ASr[:, :, j-1], ASi[:, :, j-1], ASr[:, :, j-1], ASi[:, :, j-1], t1[:], t2[:])
        ts(NASi[:], ASi[:], -1.0, ALU.mult)
        psb = [P.ps([128, 512]) for _ in range(8)]
        ZTr = P.sb([128, 16, 64], F32); ZTi = P.sb([128, 16, 64], F32); WI = P.sb([128, 16, 128], F32)
        dtl = T_([128, 128]); wtmp = T_([128, 128])
        for g in range(16):
            k, par = g // 2, g % 2
            ps_ = slice(par * 64, par * 64 + 64)
            b = g % 2
            for (src, dst, nm) in [(Zr, ZTr, 'ZTr'), (Zi, ZTi, 'ZTi')]:
                bb = 2 * b + (0 if src is Zr else 1)
                P.op('pe', 'transpose', [('v', id(src)), 'identf'], [('ps', bb)], out=psb[bb][:, 0:64], in_=src[ps_, k, :, :].rearrange("p s c -> p (s c)"), identity=ident[ps_, ps_])
                P.op('act', 'copy', [('ps', bb)], [nm], out=dst[:, g, :], in_=psb[bb][:, 0:64])
            bb = 4 + b
            P.op('pe', 'matmul', [('v', id(Xr)), ('v', id(Yr))], [('ps', bb)], psb[bb][:, 0:128], lhsT=Xr[ps_, k, :, :].rearrange("p s c -> p (s c)"), rhs=Yr[ps_, k, :, :].rearrange("p s c -> p (s c)"), start=True, stop=False)
            P.op('pe', 'matmul', [('v', id(Xi)), ('v', id(NYi))], [('ps', bb)], psb[bb][:, 0:128], lhsT=Xi[ps_, k, :, :].rearrange("p s c -> p (s c)"), rhs=NYi[ps_, k, :, :].rearrange("p s c -> p (s c)"), start=False, stop=True)
            P.dma('sp', 'dtl', [], ['dtl'], out=dtl[:], in_=dtile[g, :].partition_broadcast(128))
            P.op('dve', 'tensor_tensor', [('ps', bb), 'cmask'], ['wtmp'], out=wtmp[:], in0=psb[bb][:, 0:128], in1=cmask[:].rearrange("p i c -> p (i c)"), op=ALU.mult)
            P.op('dve', 'tensor_tensor', ['dtl', 'identf'], ['dtl'], out=dtl[:], in0=dtl[:], in1=ident[:], op=ALU.mult)
            P.op('dve', 'tensor_tensor', ['wtmp', 'dtl'], ['WI'], out=WI[:, g, :], in0=wtmp[:], in1=dtl[:], op=ALU.add)
        U = [[P.sb([128, M], F32) for _ in range(2)] for _ in range(2)]
        S = [[P.sb([128, M], F32) for _ in range(2)] for _ in range(2)]
        Sb = [P.sb([128, M], F32) for _ in range(2)]
        Yst = [P.sb([128, M], F32) for _ in range(2)]
        for k in range(8):
            us = k % 2
            for par in range(2):
                P.dma('sp', ('U', us, par), [], [('U', us, par)], out=U[us][par][:], in_=ud[2*k+par].rearrange("s c m -> (s c) m"))
            for mc in range(M // MC):
                msl = slice(mc * MC, (mc + 1) * MC)
                for ri, ZT, nm in [(0, ZTr, 'ZTr'), (1, ZTi, 'ZTi')]:
                    bb = ri
                    for par in range(2):
                        P.op('pe', 'matmul', [nm, ('U', us, par)], [('ps', bb)], psb[bb][par*64:(par+1)*64, 0:MC], lhsT=ZT[:, 2*k+par, :], rhs=U[us][par][:, msl], start=True, stop=True)
                    P.op('act', 'copy', [('ps', bb)], [('S', 0, ri)], out=S[0][ri][:, msl], in_=psb[bb][:, 0:MC])
            cur = 0
            for j in range(NST):
                d = 1 << j
                nx = 1 - cur
                c_r, c_i, n_r, n_i = S[cur][0], S[cur][1], S[nx][0], S[nx][1]
                R = [('S', cur, 0), ('S', cur, 1), ('v', id(ASr)), ('v', id(ASi)), ('v', id(NASi))]
                P.op('dve', 'scalar_tensor_tensor', R, [('S', nx, 0)], out=n_r[:, d:M], in0=c_r[:, 0:M-d], scalar=ASr[:, k, j:j+1], in1=c_r[:, d:M], op0=ALU.mult, op1=ALU.add)
                P.op('dve', 'scalar_tensor_tensor', R + [('S', nx, 0)], [('S', nx, 0)], out=n_r[:, d:M], in0=c_i[:, 0:M-d], scalar=NASi[:, k, j:j+1], in1=n_r[:, d:M], op0=ALU.mult, op1=ALU.add)
                P.op('dve', 'scalar_tensor_tensor', R, [('S', nx, 1)], out=n_i[:, d:M], in0=c_i[:, 0:M-d], scalar=ASr[:, k, j:j+1], in1=c_i[:, d:M], op0=ALU.mult, op1=ALU.add)
                P.op('dve', 'scalar_tensor_tensor', R + [('S', nx, 1)], [('S', nx, 1)], out=n_i[:, d:M], in0=c_r[:, 0:M-d], scalar=ASi[:, k, j:j+1], in1=n_i[:, d:M], op0=ALU.mult, op1=ALU.add)
                P.op('pool', 'tensor_copy', [('S', cur, 0)], [('S', nx, 0)], out=n_r[:, 0:d], in_=c_r[:, 0:d])
                P.op('pool', 'tensor_copy', [('S', cur, 1)], [('S', nx, 1)], out=n_i[:, 0:d], in_=c_i[:, 0:d])
                cur = nx
            for ri in range(2):
                P.op('pool', 'memset', [], [('Sb', ri)], Sb[ri][:, 0:1], 0.0)
                P.op('pool', 'tensor_copy', [('S', cur, ri)], [('Sb', ri)], out=Sb[ri][:, 1:M], in_=S[cur][ri][:, 0:M-1])
            for par in range(2):
                g = 2 * k + par
                ps_ = slice(par * 64, par * 64 + 64)
                for mc in range(M // MC):
                    msl = slice(mc * MC, (mc + 1) * MC)
                    bb = 2 + (mc + par) % 6
                    P.op('pe', 'matmul', ['WI', ('U', us, par)], [('ps', bb)], psb[bb][:, 0:MC], lhsT=WI[:, g, :], rhs=U[us][par][:, msl], start=True, stop=False)
                    P.op('pe', 'matmul', ['Ybr', ('Sb', 0)], [('ps', bb)], psb[bb][:, 0:MC], lhsT=Ybr[ps_, k, :, :].rearrange("p s c -> p (s c)"), rhs=Sb[0][ps_, msl], start=False, stop=False)
                    P.op('pe', 'matmul', ['Ybn', ('Sb', 1)], [('ps', bb)], psb[bb][:, 0:MC], lhsT=Ybn[ps_, k, :, :].rearrange("p s c -> p (s c)"), rhs=Sb[1][ps_, msl], start=False, stop=True)
                    P.op('act', 'copy', [('ps', bb)], [('Yst', par)], out=Yst[par][:, msl], in_=psb[bb][:, 0:MC])
                P.dma('sp', ('Yst', par), [('Yst', par)], [], out=yd[g].rearrange("i c m -> (i c) m"), in_=Yst[par][:])
        P.emit()


def s5_phaseC(nc, T, yd, yT):
    M = T // 8
    MB = min(512, M)
    with ExitStack() as st:
        P = Prog(nc, st)
        Yin = [P.sb([128, 8, MB], F32) for _ in range(2)]
        Yo = [P.sb([128, MB, 8], F32) for _ in range(2)]
        n = 0
        for ct in range(2):
            for mb in range(M // MB):
                s = n % 2; n += 1
                for gl in range(8):
                    g = ct * 8 + gl
                    P.dma('sp', ('Yin', s), [], [('Yin', s)], out=Yin[s][16*gl:16*gl+16, :, :], in_=yd[g, :, :, mb*MB:(mb+1)*MB].rearrange("i c m -> c i m"))
                P.op('dve' if s == 0 else 'pool', 'tensor_copy', [('Yin', s)], [('Yo', s)], out=Yo[s][:].rearrange("p m i -> p i m"), in_=Yin[s][:])
                P.dma('sp', ('Yo', s), [('Yo', s)], [], out=yT[ct*128:(ct+1)*128, mb*MB*8:(mb+1)*MB*8], in_=Yo[s][:].rearrange("p m i -> p (m i)"))
        P.emit()


D = 1024; FH = 2816; NH = 22
EPS = 1e-6


def p1_stage(nc, n_tok, kind, xT, mT, w, x1T):
    TT = 512
    WO = 1024 if kind == 'nsa' else 2048
    with ExitStack() as st:
        P = Prog(nc, st)
        w_sb = P.sb([128, 8, WO], BF16)
        xt = [P.sb([128, 8, TT], F32) for _ in range(2)]
        xo = [P.sb([128, 8, TT], F32) for _ in range(2)]
        if kind == 'nsa':
            mt = [P.sb([128, 8, TT], BF16) for _ in range(2)]
        else:
            yt = [P.sb([128, 8, TT], F32) for _ in range(2)]
            z = P.sb([128, 8, TT], BF16)
            sg = [P.sb([128, TT], F32) for _ in range(2)]
            tm = [P.sb([128, TT], F32) for _ in range(2)]
        psb = [P.ps([128, 512]) for _ in range(8)]
        for c in range(8):
            for hf in range(WO // 1024):
                P.dma('pool', 'w', [], ['w'], out=w_sb[:, c, hf*1024:(hf+1)*1024], in_=w[c*128:(c+1)*128, hf*1024:(hf+1)*1024])
        xTv = xT.rearrange("(c p) t -> p c t", p=128); mTv = mT.rearrange("(c p) t -> p c t", p=128); oTv = x1T.rearrange("(c p) t -> p c t", p=128)
        nt = n_tok // TT

        def load(i):
            s = i % 2
            sl = slice(i*TT, (i+1)*TT)
            P.dma('sp', ('xt', s), [], [('xt', s)], out=xt[s][:], in_=xTv[:, :, sl])
            if kind == 'nsa':
                for c in range(8):
                    P.dma('pool', ('mt', s), [], [('mt', s)], out=mt[s][:, c, :], in_=mTv[:, c, sl])
            else:
                P.dma('sp', ('yt', s), [], [('yt', s)], out=yt[s][:], in_=mTv[:, :, sl])
        load(0)
        kb = 0
        for i in range(nt):
            s = i % 2
            if i + 1 < nt:
                load(i + 1)
            if kind == 's5':
                P.op('act', 'activation', [('yt', s)], ['z'], out=z[:], in_=yt[s][:], func=AF.Gelu_apprx_tanh)
            for fc in range(8):
                if kind == 'nsa':
                    b = kb % 8; kb += 1
                    for kc in range(8):
                        P.op('pe', 'matmul', ['w', ('mt', s)], [('ps', b)], psb[b][:], lhsT=w_sb[:, kc, fc*128:(fc+1)*128], rhs=mt[s][:, kc, :], start=(kc == 0), stop=(kc == 7))
                    P.op('dve', 'tensor_tensor', [('xt', s), ('ps', b)], [('xo', s)], out=xo[s][:, fc, :], in0=xt[s][:, fc, :], in1=psb[b][:], op=ALU.add)
                else:
                    bv = kb % 8; bg = (kb + 1) % 8; kb += 2
                    for kc in range(8):
                        P.op('pe', 'matmul', ['w', 'z'], [('ps', bv)], psb[bv][:], lhsT=w_sb[:, kc, fc*128:(fc+1)*128], rhs=z[:, kc, :], start=(kc == 0), stop=(kc == 7))
                    for kc in range(8):
                        P.op('pe', 'matmul', ['w', 'z'], [('ps', bg)], psb[bg][:], lhsT=w_sb[:, kc, 1024+fc*128:1024+(fc+1)*128], rhs=z[:, kc, :], start=(kc == 0), stop=(kc == 7))
                    k = fc % 2
                    P.op('act', 'activation', [('ps', bg)], [('sg', k)], out=sg[k][:], in_=psb[bg][:], func=AF.Sigmoid)
                    P.op('dve', 'tensor_tensor', [('sg', k), ('ps', bv)], [('tm', k)], out=tm[k][:], in0=sg[k][:], in1=psb[bv][:], op=ALU.mult)
                    P.op('pool', 'tensor_tensor', [('xt', s), ('tm', k)], [('xo', s)], out=xo[s][:, fc, :], in0=xt[s][:, fc, :], in1=tm[k][:], op=ALU.add)
            P.dma('sp', ('xo', s), [('xo', s)], [], out=oTv[:, :, i*TT:(i+1)*TT], in_=xo[s][:])
        P.emit()


def ffn_stage(nc, n_tok, xT, gn, wg, wu, wd, oT):
    TT = 256
    with ExitStack() as st:
        P = Prog(nc, st)
        wg_sb = P.sb([128, 8, FH], BF16); wu_sb = P.sb([128, 8, FH], BF16)
        wd_sb = P.sb([128, NH, D], BF16)
        g_sb = P.sb([128, 8], F32); ones = P.sb([128, 128], BF16)
        xt = [P.sb([128, 8, TT], F32) for _ in range(2)]
        xo = [P.sb([128, 8, TT], F32) for _ in range(2)]
        xsq = P.sb([128, 8, TT], BF16); h = P.sb([128, 8, TT], BF16)
        rs = P.sb([128, TT], F32); rstd = P.sb([128, TT], F32)
        sg = [P.sb([128, TT], BF16) for _ in range(2)]
        a = P.sb([128, NH, TT], BF16)
        psb = [P.ps([128, 512]) for _ in range(8)]
        pi = [0]

        def bank():
            b = pi[0] % 8; pi[0] += 1
            return b
        P.op('pool', 'memset', [], ['ones'], ones[:], 1.0)
        P.dma('sp', 'g', [], ['g'], out=g_sb[:], in_=gn[:, :])
        HW = FH // 2
        for c in range(8):
            for hf in range(2):
                P.dma('pool', 'wg', [], ['wg'], out=wg_sb[:, c, hf*HW:(hf+1)*HW], in_=wg[c*128:(c+1)*128, hf*HW:(hf+1)*HW])
                P.dma('pool', 'wu', [], ['wu'], out=wu_sb[:, c, hf*HW:(hf+1)*HW], in_=wu[c*128:(c+1)*128, hf*HW:(hf+1)*HW])
        for hc in range(NH):
            P.dma('pool', 'wd', [], ['wd'], out=wd_sb[:, hc, :], in_=wd[hc*128:(hc+1)*128, :])
        xTv = xT.rearrange("(c p) t -> p c t", p=128)
        oTv = oT.rearrange("(c p) t -> p c t", p=128)
        nt = n_tok // TT

        def load(i):
            s = i % 2
            P.dma('sp', ('xt', s), [], [('xt', s)], out=xt[s][:], in_=xTv[:, :, i*TT:(i+1)*TT])
        load(0)
        for i in range(nt):
            s = i % 2
            if i + 1 < nt:
                load(i + 1)
            X = xt[s]
            P.op('act', 'activation', [('xt', s)], ['xsq'], out=xsq[:], in_=X[:], func=AF.Square)
            b = bank()
            for c in range(8):
                P.op('pe', 'matmul', ['xsq', 'ones'], [('ps', b)], psb[b][:, :TT], lhsT=ones[:], rhs=xsq[:, c, :], start=(c == 0), stop=(c == 7))
            P.op('act', 'activation', [('ps', b)], ['rs'], out=rs[:], in_=psb[b][:, :TT], func=AF.Sqrt, scale=1.0/D, bias=EPS)
            P.op('dve', 'reciprocal', ['rs'], ['rstd'], out=rstd[:], in_=rs[:])
            for c in range(8):
                P.op('dve', 'scalar_tensor_tensor', [('xt', s), 'g', 'rstd'], [('h', c)], out=h[:, c, :], in0=X[:, c, :], scalar=g_sb[:, c:c+1], in1=rstd[:], op0=ALU.mult, op1=ALU.mult)
            for hc in range(NH):
                bg = bank(); bu = bank()
                for c in range(8):
                    P.op('pe', 'matmul', ['wg', ('h', c)], [('ps', bg)], psb[bg][:, :TT], lhsT=wg_sb[:, c, hc*128:(hc+1)*128], rhs=h[:, c, :], start=(c == 0), stop=(c == 7))
                for c in range(8):
                    P.op('pe', 'matmul', ['wu', ('h', c)], [('ps', bu)], psb[bu][:, :TT], lhsT=wu_sb[:, c, hc*128:(hc+1)*128], rhs=h[:, c, :], start=(c == 0), stop=(c == 7))
                k = hc % 2
                P.op('act', 'activation', [('ps', bg)], [('sg', k)], out=sg[k][:], in_=psb[bg][:, :TT], func=AF.Silu)
                P.op('dve', 'tensor_tensor', [('sg', k), ('ps', bu)], [('a', hc)], out=a[:, hc, :], in0=sg[k][:], in1=psb[bu][:, :TT], op=ALU.mult)
            for fc in range(8):
                b = bank()
                for hc in range(NH):
                    P.op('pe', 'matmul', ['wd', ('a', hc)], [('ps', b)], psb[b][:, :TT], lhsT=wd_sb[:, hc, fc*128:(fc+1)*128], rhs=a[:, hc, :], start=(hc == 0), stop=(hc == NH-1))
                P.op('dve', 'tensor_tensor', [('xt', s), ('ps', b)], [('xo', s)], out=xo[s][:, fc, :], in0=X[:, fc, :], in1=psb[b][:, :TT], op=ALU.add)
            P.dma('sp', ('xo', s), [('xo', s)], [], out=oTv[:, :, i*TT:(i+1)*TT], in_=xo[s][:])
        P.emit()


SEQ = 16384
BATCH = 2
NTOK = SEQ // 4


def build_nsa(T):
    nc = bass.Bass("TRN2", target_bir_lowering=False)
    NCP = T // 16
    di = lambda n, s, dt=F32: nc.dram_tensor(n, s, dt, kind="ExternalInput").ap()
    xT = di("xT", [D, T]); gm = di("gm", [128, 8]); w_in = di("w_in", [D, NQC]); qg = di("qg", [64]); kg = di("kg", [3, 64])
    posT = di("posT", [2, 64, 32]); w1 = di("w1", [2, 2048, 256]); b1T = di("b1T", [2, 128, 2]); w2 = di("w2", [2, 256, 64]); b2 = di("b2", [2, 64])
    oT = nc.dram_tensor("oT", [256, T], F32, kind="ExternalOutput").ap()
    sc = lambda n, s, dt: nc.dram_tensor(n, s, dt, kind="Internal").ap()
    featT_d = sc("featT", [8, 64, T], BF16); vtok_d = sc("vtok", [T, 2, 64], BF16); gates_d = sc("gates", [T, 12], F32)
    kcc_d = sc("kcc", [64, NCP], BF16); vcc_d = sc("vcc", [NCP, 64], BF16)
    nsa_phaseA(nc, T, xT, gm, w_in, qg, kg, featT_d, vtok_d, gates_d)
    nsa_phaseB(nc, T, featT_d, posT, w1, b1T, w2, b2, kg, kcc_d, vcc_d)
    nsa_phaseC(nc, T, featT_d, vtok_d, gates_d, kcc_d, vcc_d, oT)
    return nc


def build_s5(T):
    nc = bass.Bass("TRN2", target_bir_lowering=False)
    M = T // 8
    di = lambda n, s, dt=F32: nc.dram_tensor(n, s, dt, kind="ExternalInput").ap()
    xT = di("xT", [D, T]); gm = di("gm", [128, 8]); w_in = di("w_in", [D, 256])
    arP = di("arP", [128, 8]); aiP = di("aiP", [128, 8]); ldtP = di("ldtP", [128, 8])
    bPr = di("bPr", [128, 8, 16]); bPi = di("bPi", [128, 8, 16]); cPr = di("cPr", [128, 8, 16]); cPi = di("cPi", [128, 8, 16]); dtile = di("dtile", [16, 128])
    yT = nc.dram_tensor("oT", [256, T], F32, kind="ExternalOutput").ap()
    ud = nc.dram_tensor("ud", [16, 8, 16, M], F32, kind="Internal").ap()
    yd = nc.dram_tensor("yd", [16, 8, 16, M], F32, kind="Internal").ap()
    s5_phaseA(nc, T, xT, gm, w_in, ud)
    s5_phaseB(nc, T, ud, arP, aiP, ldtP, bPr, bPi, cPr, cPi, dtile, yd)
    s5_phaseC(nc, T, yd, yT)
    return nc


def build_post(n_tok, kind):
    nc = bass.Bass("TRN2", target_bir_lowering=False)
    di = lambda n, s, dt=F32: nc.dram_tensor(n, s, dt, kind="ExternalInput").ap()
    xT = di("xT", [D, n_tok]); mT = di("mT", [D, n_tok]); w = di("w", [D, 1024 if kind == 'nsa' else 2048])
    gn = di("gn", [128, 8]); wg = di("wg", [D, FH]); wu = di("wu", [D, FH]); wd = di("wd", [FH, D])
    oT = nc.dram_tensor("oT", [D, n_tok], F32, kind="ExternalOutput").ap()
    x1T = nc.dram_tensor("x1T", [D, n_tok], F32, kind="Internal").ap()
    p1_stage(nc, n_tok, kind, xT, mT, w, x1T)
    ffn_stage(nc, n_tok, x1T, gn, wg, wu, wd, oT)
    return nc


def _c(a):
    return np.ascontiguousarray(a, dtype=np.float32)


def nsa_inputs_for_group(inp, i, layer, g):
    w = inp["nsa_w_in"][i]
    cols = [w[:, 256*g:256*g+256]] + [w[:, 1024 + k*256 + 64*g: 1024 + k*256 + 64*g + 64] for k in range(6)] + [w[:, 2560 + 12*g: 2560 + 12*g + 12]]
    return {
        "w_in": _c(np.concatenate(cols, axis=1)),
        "qg": _c(inp["nsa_q_gain"][i]), "kg": _c(inp["nsa_k_gain"][i]),
        "posT": _c(inp["nsa_cmp_pos"][i].transpose(0, 2, 1)),
        "w1": _c(inp["nsa_cmp_w1"][i]), "b1T": _c(inp["nsa_cmp_b1"][i].reshape(2, 2, 128).transpose(0, 2, 1)),
        "w2": _c(inp["nsa_cmp_w2"][i]), "b2": _c(inp["nsa_cmp_b2"][i]),
        "gm": _c(inp["mix_norm"][layer].reshape(8, 128).T),
    }


def s5_inputs_for_core(inp, i, layer, q):
    gs = slice(16*q, 16*q+16)
    vp = lambda a: _c(a.reshape(8, 2, 64).transpose(1, 2, 0).reshape(128, 8))
    bp = lambda a: _c(a.reshape(8, 2, 64, 16).transpose(1, 2, 0, 3).reshape(128, 8, 16))
    d = inp["s5_d"][i].reshape(64, 16)[gs]
    return {
        "w_in": _c(inp["s5_w_in"][i][:, 256*q:256*q+256]),
        "gm": _c(inp["mix_norm"][layer].reshape(8, 128).T),
        "arP": vp(inp["s5_a_re"][i][gs]), "aiP": vp(inp["s5_a_im"][i][gs]),
        "ldtP": vp(np.repeat(inp["s5_log_dt"][i][gs][:, None], 64, axis=1)),
        "bPr": bp(inp["s5_b_re"][i][gs]), "bPi": bp(inp["s5_b_im"][i][gs]),
        "cPr": bp(inp["s5_c_re"][i][gs].transpose(0, 2, 1)), "cPi": bp(inp["s5_c_im"][i][gs].transpose(0, 2, 1)),
        "dtile": _c(np.tile(d, (1, 8))),
    }


_PROGS = {}


def _prog(name, fn):
    if name not in _PROGS:
        _PROGS[name] = fn()
    return _PROGS[name]


def kernel(**inputs):
    inp = {k: np.asarray(v) for k, v in inputs.items()}
    x = inp["x"]
    T = x.shape[1]
    cores = list(range(8))
    XT = [_c(x[b].T) for b in range(BATCH)]
    for layer in range(4):
        i = layer // 2
        if layer % 2 == 0:
            nc = _prog("nsa", lambda: build_nsa(T))
            in_maps = []
            for c in cores:
                b, g = divmod(c, 4)
                m = nsa_inputs_for_group(inp, i, layer, g)
                m["xT"] = XT[b]
                in_maps.append(m)
            wmix = _c(inp["nsa_w_out"][i]); kind = 'nsa'
        else:
            nc = _prog("s5", lambda: build_s5(T))
            in_maps = []
            for c in cores:
                b, q = divmod(c, 4)
                m = s5_inputs_for_core(inp, i, layer, q)
                m["xT"] = XT[b]
                in_maps.append(m)
            wmix = _c(inp["s5_w_glu"][i]); kind = 's5'
        res = run_bass_kernel_spmd(nc, in_maps, core_ids=cores).results
        MT = [np.concatenate([np.asarray(res[b*4 + g]["oT"]) for g in range(4)], axis=0) for b in range(BATCH)]
        nt = T // 4
        ncp = _prog("post_" + kind, lambda: build_post(nt, kind))
        gn = _c(inp["ffn_norm"][layer].reshape(8, 128).T)
        wg = _c(inp["ffn_w_gate"][layer]); wu = _c(inp["ffn_w_up"][layer]); wd = _c(inp["ffn_w_down"][layer])
        in_maps = []
        for c in cores:
            b, qt = divmod(c, 4)
            sl = slice(qt*nt, (qt+1)*nt)
            in_maps.append({"xT": _c(XT[b][:, sl]), "mT": _c(MT[b][:, sl]), "w": wmix, "gn": gn, "wg": wg, "wu": wu, "wd": wd})
        res = run_bass_kernel_spmd(ncp, in_maps, core_ids=cores).results
        XT = [np.concatenate([np.asarray(res[b*4 + qt]["oT"]) for qt in range(4)], axis=1) for b in range(BATCH)]
    return np.ascontiguousarray(np.stack([XT[b].T for b in range(BATCH)], axis=0), dtype=np.float32)
```
